# Optimizing a Trainium2 kernel written in Bass

```python
import math
import jax, jax.numpy as jnp
from jax import lax
import numpy as np

D_MODEL = 1024
BATCH = 8
SEQ = 2048
DEPTH = 4
DEC_BATCH = 128
DEC_SEQ = 4
PAST_LEN = 16384
PAGE_SIZE = 128

N_META = 16
D_MIX = D_MODEL
LRU_WIDTH = D_MIX // 2
LRU_HEADS = 8
LRU_HEAD_DIM = LRU_WIDTH // LRU_HEADS
LRU_C = 8.0
CONV_K = 4
SSD_INNER = D_MIX - LRU_WIDTH
SSD_HEAD_DIM = 64
SSD_HEADS = SSD_INNER // SSD_HEAD_DIM
SSD_GROUPS = 2
SSD_STATE = 128
SSD_CONV_DIM = SSD_INNER + 2 * SSD_GROUPS * SSD_STATE
SSD_CHUNK = 128
D_FF = 3 * D_MODEL
FFN_CONV_K = 3
P_IN = 2 * LRU_WIDTH + SSD_INNER + SSD_CONV_DIM + SSD_HEADS
EPS = 1e-6

kernel_name = "hymba_rglru_ssd_convffn_step"


def rmsnorm(x, g):
    xf = x.astype(jnp.float32)
    y = xf * lax.rsqrt(jnp.mean(xf * xf, axis=-1, keepdims=True) + EPS)
    return (y * g.astype(jnp.float32)).astype(x.dtype)


def causal_dwconv(x, buf, w, b):
    K = w.shape[0]
    L = x.shape[1]
    xp = jnp.concatenate([buf.astype(x.dtype), x], axis=1)
    y = b + sum(xp[:, k:k + L] * w[k] for k in range(K))
    return y, xp[:, L:]


def rg_lru(x, h0, wa, ba, wx, bx, a_param):
    Bsz, L, W = x.shape
    xh = x.reshape(Bsz, L, LRU_HEADS, LRU_HEAD_DIM)
    r = jax.nn.sigmoid(jnp.einsum('blhi,hij->blhj', xh, wa).reshape(Bsz, L, W) + ba)
    i = jax.nn.sigmoid(jnp.einsum('blhi,hij->blhj', xh, wx).reshape(Bsz, L, W) + bx)
    log_a = -LRU_C * r.astype(jnp.float32) * jax.nn.softplus(-a_param.astype(jnp.float32))
    a = jnp.exp(log_a)
    mult = jnp.sqrt(-jnp.expm1(2.0 * log_a))
    b = mult * (i * x).astype(jnp.float32)
    b = b.at[:, 0].add(a[:, 0] * h0.astype(jnp.float32))

    def combine(lft, rgt):
        a_l, b_l = lft
        a_r, b_r = rgt
        return a_l * a_r, a_r * b_l + b_r

    _, h = lax.associative_scan(combine, (a, b), axis=1)
    return h.astype(x.dtype), h[:, -1]


def ssd_chunked(x, dt, A, Bm, Cm, h0):
    Bsz, L, H, P = x.shape
    Q = min(SSD_CHUNK, L)
    pad = (-L) % Q
    xf = x.astype(jnp.float32)
    Bf = Bm.astype(jnp.float32)
    Cf = Cm.astype(jnp.float32)
    if pad:
        pw = ((0, 0), (0, pad))
        xf = jnp.pad(xf, pw + ((0, 0), (0, 0)))
        dt = jnp.pad(dt, pw + ((0, 0),))
        Bf = jnp.pad(Bf, pw + ((0, 0), (0, 0)))
        Cf = jnp.pad(Cf, pw + ((0, 0), (0, 0)))
    nC = (L + pad) // Q
    rep = H // SSD_GROUPS
    xc = xf.reshape(Bsz, nC, Q, H, P)
    dtc = dt.reshape(Bsz, nC, Q, H)
    Bh = jnp.repeat(Bf.reshape(Bsz, nC, Q, SSD_GROUPS, -1), rep, axis=3)
    Ch = jnp.repeat(Cf.reshape(Bsz, nC, Q, SSD_GROUPS, -1), rep, axis=3)
    cs = jnp.cumsum(dtc * A.astype(jnp.float32), axis=2)
    mask = jnp.tril(jnp.ones((Q, Q), dtype=bool))[None, None, :, :, None]
    seg = cs[:, :, :, None, :] - cs[:, :, None, :, :]
    Lmat = jnp.exp(jnp.where(mask, seg, -jnp.inf))
    xdt = xc * dtc[..., None]
    scores = jnp.einsum('bcthn,bcshn->bctsh', Ch, Bh) * Lmat
    y_intra = jnp.einsum('bctsh,bcshp->bcthp', scores, xdt)
    decay_to_end = jnp.exp(cs[:, :, -1:, :] - cs)
    chunk_states = jnp.einsum('bcsh,bcshn,bcshp->bchpn', decay_to_end, Bh, xdt)
    chunk_decay = jnp.exp(cs[:, :, -1, :])

    def step(h, inp):
        dec, st = inp
        return h * dec[..., None, None] + st, h

    hT, h_in = lax.scan(step, h0.astype(jnp.float32),
                        (jnp.swapaxes(chunk_decay, 0, 1), jnp.swapaxes(chunk_states, 0, 1)))
    h_in = jnp.swapaxes(h_in, 0, 1)
    y_inter = jnp.einsum('bcthn,bchpn->bcthp', Ch, h_in) * jnp.exp(cs)[..., None]
    y = (y_intra + y_inter).reshape(Bsz, nC * Q, H, P)[:, :L]
    return y.astype(x.dtype), hT


def mixer(h, n_lead, st_lru_conv, st_lru_h, st_ssd_conv, st_ssd, p, l):
    Bsz, L, _ = h.shape
    proj = h @ p['w_in'][l]
    o1 = LRU_WIDTH
    o2 = o1 + LRU_WIDTH
    o3 = o2 + SSD_INNER
    o4 = o3 + SSD_CONV_DIM
    lru_x, lru_gate, z, xbc, dt_raw = (proj[..., :o1], proj[..., o1:o2], proj[..., o2:o3],
                                       proj[..., o3:o4], proj[..., o4:])
    xc, new_lru_conv = causal_dwconv(lru_x, st_lru_conv, p['lru_conv_w'][l], p['lru_conv_b'][l])
    hl, new_lru_h = rg_lru(xc, st_lru_h, p['lru_wa'][l], p['lru_ba'][l], p['lru_wx'][l],
                           p['lru_bx'][l], p['lru_a_param'][l])
    lru_out = rmsnorm(hl * jax.nn.gelu(lru_gate), p['lru_out_norm'][l])
    xbc_c, new_ssd_conv = causal_dwconv(xbc, st_ssd_conv, p['ssd_conv_w'][l], p['ssd_conv_b'][l])
    xbc_c = jax.nn.silu(xbc_c)
    xs = xbc_c[..., :SSD_INNER].reshape(Bsz, L, SSD_HEADS, SSD_HEAD_DIM)
    Bm = xbc_c[..., SSD_INNER:SSD_INNER + SSD_GROUPS * SSD_STATE].reshape(Bsz, L, SSD_GROUPS, SSD_STATE)
    Cm = xbc_c[..., SSD_INNER + SSD_GROUPS * SSD_STATE:].reshape(Bsz, L, SSD_GROUPS, SSD_STATE)
    dt = jax.nn.softplus((dt_raw + p['ssd_dt_bias'][l]).astype(jnp.float32))
    A = -jnp.exp(p['ssd_a_log'][l].astype(jnp.float32))
    if n_lead > 0:
        y1, h1 = ssd_chunked(xs[:, :n_lead], dt[:, :n_lead], A, Bm[:, :n_lead], Cm[:, :n_lead], st_ssd)
        y2, new_ssd = ssd_chunked(xs[:, n_lead:], dt[:, n_lead:], A, Bm[:, n_lead:], Cm[:, n_lead:], h1)
        ys = jnp.concatenate([y1, y2], axis=1)
    else:
        ys, new_ssd = ssd_chunked(xs, dt, A, Bm, Cm, st_ssd)
    ys = ys + xs * p['ssd_d'][l][:, None]
    ys = ys.reshape(Bsz, L, SSD_INNER)
    ssd_out = rmsnorm(ys * jax.nn.silu(z), p['ssd_out_norm'][l])
    out = jnp.concatenate([lru_out, ssd_out], axis=-1) @ p['w_out'][l]
    return out, new_lru_conv, new_lru_h, new_ssd_conv, new_ssd


def conv_ffn(h, buf, w_up, cw, cb, w_down):
    u = h @ w_up
    u, new_buf = causal_dwconv(u, buf, cw, cb)
    g, v = u[..., :D_FF], u[..., D_FF:]
    return (jax.nn.gelu(g) * v) @ w_down, new_buf


def trunk(x, n_lead, st_lru_conv, st_lru_h, st_ssd_conv, st_ssd, st_ffn_conv, p):
    o_lc, o_lh, o_sc, o_ss, o_fc = [], [], [], [], []
    for l in range(DEPTH):
        h = rmsnorm(x, p['norm_mix'][l])
        m, lc, lh, sc, ss = mixer(h, n_lead, st_lru_conv[l], st_lru_h[l], st_ssd_conv[l], st_ssd[l], p, l)
        x = x + m
        h = rmsnorm(x, p['norm_ffn'][l])
        f, fc = conv_ffn(h, st_ffn_conv[l], p['ffn_w_up'][l], p['ffn_conv_w'][l],
                         p['ffn_conv_b'][l], p['ffn_w_down'][l])
        x = x + f
        o_lc.append(lc); o_lh.append(lh); o_sc.append(sc); o_ss.append(ss); o_fc.append(fc)
    y = rmsnorm(x, p['norm_final'])
    return y, jnp.stack(o_lc), jnp.stack(o_lh), jnp.stack(o_sc), jnp.stack(o_ss), jnp.stack(o_fc)


def setup_inputs(seed: int = 0) -> dict:
    key = jax.random.key(seed)
    ks = iter(jax.random.split(key, 40))
    nrm = lambda shape, s: jax.random.normal(next(ks), shape, jnp.float32) * s
    gain = lambda shape: 1.0 + nrm(shape, 0.02)
    a0 = jax.random.uniform(next(ks), (DEPTH, LRU_WIDTH), jnp.float32, 0.9, 0.999) ** (1.0 / LRU_C)
    lru_a_param = jnp.log(a0) - jnp.log1p(-a0)
    dt0 = jnp.exp(jax.random.uniform(next(ks), (DEPTH, SSD_HEADS), jnp.float32, math.log(1e-3), math.log(1e-1)))
    ssd_dt_bias = dt0 + jnp.log(-jnp.expm1(-dt0))
    ssd_a_log = jnp.log(jax.random.uniform(next(ks), (DEPTH, SSD_HEADS), jnp.float32, 1.0, 16.0))
    return {
        'x_prompt': nrm((BATCH, SEQ, D_MODEL), 1.0),
        'x_sample': nrm((DEC_BATCH, DEC_SEQ, D_MODEL), 1.0),
        'state_lru_conv': nrm((DEPTH, DEC_BATCH, CONV_K - 1, LRU_WIDTH), 1.0),
        'state_lru_h': nrm((DEPTH, DEC_BATCH, LRU_WIDTH), 0.5),
        'state_ssd_conv': nrm((DEPTH, DEC_BATCH, CONV_K - 1, SSD_CONV_DIM), 1.0),
        'state_ssd': nrm((DEPTH, DEC_BATCH, SSD_HEADS, SSD_HEAD_DIM, SSD_STATE), 0.3),
        'state_ffn_conv': nrm((DEPTH, DEC_BATCH, FFN_CONV_K - 1, 2 * D_FF), 1.0),
        'meta_tokens': nrm((N_META, D_MODEL), 1.0),
        'norm_mix': gain((DEPTH, D_MODEL)),
        'w_in': nrm((DEPTH, D_MODEL, P_IN), D_MODEL ** -0.5),
        'lru_conv_w': nrm((DEPTH, CONV_K, LRU_WIDTH), CONV_K ** -0.5),
        'lru_conv_b': nrm((DEPTH, LRU_WIDTH), 0.02),
        'lru_wa': nrm((DEPTH, LRU_HEADS, LRU_HEAD_DIM, LRU_HEAD_DIM), LRU_HEAD_DIM ** -0.5),
        'lru_ba': nrm((DEPTH, LRU_WIDTH), 0.02),
        'lru_wx': nrm((DEPTH, LRU_HEADS, LRU_HEAD_DIM, LRU_HEAD_DIM), LRU_HEAD_DIM ** -0.5),
        'lru_bx': nrm((DEPTH, LRU_WIDTH), 0.02),
        'lru_a_param': lru_a_param,
        'lru_out_norm': gain((DEPTH, LRU_WIDTH)),
        'ssd_conv_w': nrm((DEPTH, CONV_K, SSD_CONV_DIM), CONV_K ** -0.5),
        'ssd_conv_b': nrm((DEPTH, SSD_CONV_DIM), 0.02),
        'ssd_dt_bias': ssd_dt_bias,
        'ssd_a_log': ssd_a_log,
        'ssd_d': gain((DEPTH, SSD_HEADS)),
        'ssd_out_norm': gain((DEPTH, SSD_INNER)),
        'w_out': nrm((DEPTH, D_MIX, D_MODEL), D_MIX ** -0.5),
        'norm_ffn': gain((DEPTH, D_MODEL)),
        'ffn_w_up': nrm((DEPTH, D_MODEL, 2 * D_FF), D_MODEL ** -0.5),
        'ffn_conv_w': nrm((DEPTH, FFN_CONV_K, 2 * D_FF), FFN_CONV_K ** -0.5),
        'ffn_conv_b': nrm((DEPTH, 2 * D_FF), 0.02),
        'ffn_w_down': nrm((DEPTH, D_FF, D_MODEL), D_FF ** -0.5),
        'norm_final': gain((D_MODEL,)),
    }


def reference(x_prompt, x_sample, state_lru_conv, state_lru_h, state_ssd_conv, state_ssd, state_ffn_conv,
              meta_tokens, norm_mix, w_in, lru_conv_w, lru_conv_b, lru_wa, lru_ba, lru_wx, lru_bx,
              lru_a_param, lru_out_norm, ssd_conv_w, ssd_conv_b, ssd_dt_bias, ssd_a_log, ssd_d,
              ssd_out_norm, w_out, norm_ffn, ffn_w_up, ffn_conv_w, ffn_conv_b, ffn_w_down, norm_final):
    p = dict(norm_mix=norm_mix, w_in=w_in, lru_conv_w=lru_conv_w, lru_conv_b=lru_conv_b,
             lru_wa=lru_wa, lru_ba=lru_ba, lru_wx=lru_wx, lru_bx=lru_bx, lru_a_param=lru_a_param,
             lru_out_norm=lru_out_norm, ssd_conv_w=ssd_conv_w, ssd_conv_b=ssd_conv_b,
             ssd_dt_bias=ssd_dt_bias, ssd_a_log=ssd_a_log, ssd_d=ssd_d, ssd_out_norm=ssd_out_norm,
             w_out=w_out, norm_ffn=norm_ffn, ffn_w_up=ffn_w_up, ffn_conv_w=ffn_conv_w,
             ffn_conv_b=ffn_conv_b, ffn_w_down=ffn_w_down, norm_final=norm_final)
    bp = x_prompt.shape[0]
    dtp = x_prompt.dtype
    xp = jnp.concatenate([jnp.broadcast_to(meta_tokens.astype(dtp)[None], (bp, N_META, D_MODEL)), x_prompt], axis=1)
    z_lc = jnp.zeros((DEPTH, bp, CONV_K - 1, LRU_WIDTH), dtp)
    z_lh = jnp.zeros((DEPTH, bp, LRU_WIDTH), jnp.float32)
    z_sc = jnp.zeros((DEPTH, bp, CONV_K - 1, SSD_CONV_DIM), dtp)
    z_ss = jnp.zeros((DEPTH, bp, SSD_HEADS, SSD_HEAD_DIM, SSD_STATE), jnp.float32)
    z_fc = jnp.zeros((DEPTH, bp, FFN_CONV_K - 1, 2 * D_FF), dtp)
    yp, p_lru_conv, p_lru_h, p_ssd_conv, p_ssd, p_ffn_conv = trunk(xp, N_META, z_lc, z_lh, z_sc, z_ss, z_fc, p)
    y_prompt = yp[:, N_META:]
    y_sample, s_lru_conv, s_lru_h, s_ssd_conv, s_ssd, s_ffn_conv = trunk(
        x_sample, 0, state_lru_conv, state_lru_h, state_ssd_conv, state_ssd, state_ffn_conv, p)
    return (y_prompt, y_sample, p_lru_conv, p_lru_h, p_ssd_conv, p_ssd, p_ffn_conv,
            s_lru_conv, s_lru_h, s_ssd_conv, s_ssd, s_ffn_conv)
```

```python
import numpy as np
import concourse.bass as bass
import concourse.mybir as mybir
from concourse.bass_utils import run_bass_kernel_spmd
from contextlib import ExitStack

F32 = mybir.dt.float32
BF16 = mybir.dt.bfloat16
ALU = mybir.AluOpType
AF = mybir.ActivationFunctionType

NCORES = 8
D = 1024
DEPTH = 4
SEQ = 2048
NMETA = 16
DB = 16
DS = 4
LW = 512
SI = 512
NH = 8
HP = 64
SN = 128
DFF = 3072
PIN = 2568
EPS = 1e-6
PT = 512
WSLOTS = 4
_DBG_TILES = None
_DBG_LAYERS = DEPTH
_DBG_PRINT = False


class Tk:
    __slots__ = ("w", "r", "excl")

    def __init__(self, excl=False):
        self.w = None
        self.r = {}
        self.excl = excl


class Eng:
    def __init__(self, idx, name, eng, sem):
        self.idx = idx
        self.name = name
        self.eng = eng
        self.sem = sem
        self.cnt = 0
        self.seen = {}


class K:
    def __init__(self, nc, es, n_dma_sems=8):
        self.nc = nc
        self.engs = []

        def mk(name, eng):
            sem = es.enter_context(nc.semaphore(name))
            e = Eng(len(self.engs), name, eng, sem)
            self.engs.append(e)
            return e

        self.pe = mk("s_pe", nc.tensor)
        self.act = mk("s_act", nc.scalar)
        self.dve = mk("s_dve", nc.vector)
        self.pool = mk("s_pool", nc.gpsimd)
        self.sp = mk("s_sp", nc.sync)
        self.dq = {}
        for q, n_ in ((self.sp, n_dma_sems), (self.pool, 24)):
            self.dq[q.idx] = [mk(f"d_{q.name}_{i}", None) for i in range(n_)]
        self.dq_rr = {q: 0 for q in self.dq}
        self.est = {}
        self.eng_free = [0.0] * len(self.engs)
        self.last_fin = 0.0
        self.cost = {self.pe.idx: 0.3, self.act.idx: 0.8, self.dve.idx: 0.9, self.pool.idx: 1.6, self.sp.idx: 0.1}

    def _deps(self, reads, writes, me=None):
        deps = {}
        for t in reads:
            if t.w is not None:
                deps[t.w[0]] = max(deps.get(t.w[0], 0), t.w[1])
            if t.excl:
                for ei, c in t.r.items():
                    if ei != me:
                        deps[ei] = max(deps.get(ei, 0), c)
        for t in writes:
            if t.w is not None:
                deps[t.w[0]] = max(deps.get(t.w[0], 0), t.w[1])
            for ei, c in t.r.items():
                deps[ei] = max(deps.get(ei, 0), c)
        return deps

    def _wait(self, e, deps):
        for ei, c in deps.items():
            if ei == e.idx and e is self.pe:
                continue
            if e.seen.get(ei, 0) < c:
                e.eng.wait_ge(self.engs[ei].sem, c)
                e.seen[ei] = c

    def op(self, e, fn, reads=(), writes=()):
        deps = self._deps(reads, writes, e.idx)
        ready = max([self.est.get(kv, 0.0) for kv in deps.items()], default=0.0)
        fin = max(ready, self.eng_free[e.idx]) + self.cost[e.idx]
        self._wait(e, deps)
        ins = fn(e.eng)
        e.cnt += 1
        ins.then_inc(e.sem, 1)
        self.est[(e.idx, e.cnt)] = fin
        self.eng_free[e.idx] = fin
        self.last_fin = fin
        for t in reads:
            t.r[e.idx] = e.cnt
        for t in writes:
            t.w = (e.idx, e.cnt)
            t.r = {}
        return ins

    def fence(self, e, tks):
        self._wait(e, self._deps((), tks))

    def dma(self, q, out, in_, reads=(), writes=()):
        sems = self.dq[q.idx]
        d = sems[self.dq_rr[q.idx] % len(sems)]
        self.dq_rr[q.idx] += 1
        deps = self._deps(reads, writes)
        if d.cnt > 0:
            deps[d.idx] = max(deps.get(d.idx, 0), d.cnt)
        ready = max([self.est.get(kv, 0.0) for kv in deps.items()], default=0.0)
        fin = max(ready, self.eng_free[q.idx]) + 3.0
        self.eng_free[q.idx] = max(ready, self.eng_free[q.idx]) + 0.1
        self._wait(q, deps)
        ins = q.eng.dma_start(out=out, in_=in_)
        d.cnt += 16
        ins.then_inc(d.sem, 16)
        self.est[(d.idx, d.cnt)] = fin
        self.last_fin = fin
        for t in reads:
            t.r[d.idx] = d.cnt
        for t in writes:
            t.w = (d.idx, d.cnt)
            t.r = {}
        return ins

    def finish(self, e):
        deps = {}
        for q, sems in self.dq.items():
            for d in sems:
                if d.cnt:
                    deps[d.idx] = d.cnt
        for o in self.engs:
            if o.eng is not None and o is not e and o.cnt:
                deps[o.idx] = o.cnt
        self._wait(e, deps)


class Ring:
    def __init__(self, bufs):
        self.bufs = bufs
        self.i = 0

    def next(self):
        b = self.bufs[self.i % len(self.bufs)]
        self.i += 1
        return b


class TileCfg:
    def __init__(self, kind, NT, nseq, L, Q, col0=0):
        self.kind = kind
        self.NT = NT
        self.nseq = nseq
        self.L = L
        self.Q = Q
        self.ninst = NT // Q
        self.col0 = col0


def build_program():
    nc = bass.Bass("TRN2", target_bir_lowering=False)

    def din(name, shape, dt=F32):
        return nc.dram_tensor(name, list(shape), dt, kind="ExternalInput").ap()

    def dout(name, shape):
        return nc.dram_tensor(name, list(shape), F32, kind="ExternalOutput").ap()

    xp_d = din("xp", [SEQ, D])
    xs_d = din("xs", [DB * DS, D])
    meta_d = din("meta", [NMETA, D])
    st_lc = din("st_lc", [DEPTH, DB * 3, LW])
    st_lh = din("st_lh", [DEPTH, DB, LW])
    st_sc = din("st_sc", [DEPTH, DB * 3, 1024])
    st_ss = din("st_ss", [DEPTH, DB, SI, SN])
    st_fc = din("st_fc", [DEPTH, DB * 2, 2 * DFF])
    w_in = din("w_in", [DEPTH, D, PIN])
    w_out = din("w_out", [DEPTH, D, D])
    w_up = din("w_up", [DEPTH, D, 2 * DFF])
    w_down = din("w_down", [DEPTH, DFF, D])
    wa_d = din("wa", [DEPTH, NH, HP, HP])
    wx_d = din("wx", [DEPTH, NH, HP, HP])
    NP1024, NP512, NP6144 = 29, 44, 16
    p1024_d = din("p1024", [NP1024, 1024])
    p512_d = din("p512", [NP512, 512])
    p6144_d = din("p6144", [NP6144, 6144])
    hp_d = din("hp", [128, 64])
    ident_d = din("ident", [128, 128])
    tri_d = din("tri", [128, 128])

    yp_o = dout("yp", [SEQ, D])
    ys_o = dout("ys", [DB * DS, D])
    o_plc = dout("o_plc", [DEPTH, 3, LW])
    o_plh = dout("o_plh", [DEPTH, 1, LW])
    o_psc = dout("o_psc", [DEPTH, 3, 1024])
    o_pss = dout("o_pss", [DEPTH, SI, SN])
    o_pfc = dout("o_pfc", [DEPTH, 2, 2 * DFF])
    o_slc = dout("o_slc", [DEPTH, DB * 3, LW])
    o_slh = dout("o_slh", [DEPTH, DB, LW])
    o_ssc = dout("o_ssc", [DEPTH, DB * 3, 1024])
    o_sss = dout("o_sss", [DEPTH, DB, SI, SN])
    o_sfc = dout("o_sfc", [DEPTH, DB * 2, 2 * DFF])

    es = ExitStack()
    with es:
        k = K(nc, es)
        pe, act, dve, pool, sp = k.pe, k.act, k.dve, k.pool, k.sp

        def sb(name, shape, dt=F32):
            return es.enter_context(nc.sbuf_tensor("sb_" + name, list(shape), dt))

        def sbring(name, n, shape, dt=F32):
            return Ring([(sb(f"{name}{i}", shape, dt), Tk()) for i in range(n)])

        banks = [es.enter_context(nc.psum_tensor(f"pb{i}", [128, 512], F32)) for i in range(8)]
        mmring = Ring([(banks[i], Tk(True)) for i in range(4)])
        t_bank = [None] * 4 + [Tk(True) for _ in range(4)]
        pb_m, t_pbm = banks[7], t_bank[7]

        ident = sb("ident", [128, 128]); t_ident = Tk()
        identb = sb("identb", [128, 128], BF16); t_identb = Tk()
        tri = sb("tri", [128, 128]); t_tri = Tk()
        onesb = sb("onesb", [128, 128], BF16); t_ones = Tk()
        p1024 = sb("p1024", [128, 8, NP1024]); t_p1024 = Tk()
        p512 = sb("p512", [128, 4, NP512]); t_p512 = Tk()
        p6144 = sb("p6144", [128, 48, NP6144]); t_p6144 = Tk()
        hp = sb("hp", [128, 64]); t_hp = Tk()
        lrud = sb("lrud", [128, 4, 5 * DEPTH]); t_lrud = Tk()
        wdt = sb("wdt", [128, DEPTH, 8, 8], BF16); t_wdt = Tk()
        wbd = sb("wbd", [128, 4, 2, 128], BF16); t_wbd = Tk()

        stg_ring = sbring("stg", 2, [128, 512])

        k.dma(sp, ident[:], ident_d, writes=[t_ident])
        k.dma(sp, tri[:], tri_d, writes=[t_tri])
        k.dma(sp, hp[:], hp_d, writes=[t_hp])
        k.op(pool, lambda e: e.memset(onesb[:], 1.0), writes=[t_ones])
        k.op(act, lambda e: e.copy(out=identb[:], in_=ident[:]), reads=[t_ident], writes=[t_identb])
        k.op(pool, lambda e: e.memset(wbd[:], 0.0), writes=[t_wbd])
        for l in range(DEPTH):
            k.dma(pool, wdt[:, l], w_in[l].rearrange("(k p) n -> p k n", p=128)[:, :, 2560:2568], writes=[t_wdt])

        def load_T(src, R, C, dst_fn, dst_tks, ev=None):
            for c0 in range(0, C, 512):
                cw = min(512, C - c0)
                stg, t_stg = stg_ring.next()
                k.dma(sp, stg[0:R, 0:cw], src[:, c0:c0 + cw], writes=[t_stg])
                nch = cw // 128
                g = max(1, min(nch, 512 // R))
                for g0 in range(0, nch, g):
                    gn = min(g, nch - g0)
                    pb, t_pb = mmring.next()
                    for i in range(gn):
                        k.op(pe, lambda e: e.transpose(pb[:, i * R:(i + 1) * R], stg[0:R, (g0 + i) * 128:(g0 + i + 1) * 128], ident[0:R, 0:R]),
                             reads=[t_stg, t_ident], writes=[t_pb])
                    ch0 = c0 // 128 + g0
                    k.op(ev or dve, lambda e: e.tensor_copy(out=dst_fn(ch0, gn), in_=pb[:, 0:gn * R].rearrange("p (c r) -> p c r", r=R)),
                         reads=[t_pb], writes=dst_tks)

        def store_T(src_fn, src_tks, R, nch, dst):
            for c0 in range(0, nch, 4):
                cn = min(4, nch - c0)
                stg, t_stg = stg_ring.next()
                for g0 in range(0, cn, 4):
                    gn = min(4, cn - g0)
                    pb, t_pb = mmring.next()
                    for i in range(gn):
                        k.op(pe, lambda e: e.transpose(pb[0:R, i * 128:(i + 1) * 128], src_fn(c0 + g0 + i), ident[:]),
                             reads=list(src_tks) + [t_ident], writes=[t_pb])
                    k.op(dve, lambda e: e.tensor_copy(out=stg[0:R, g0 * 128:(g0 + gn) * 128], in_=pb[0:R, 0:gn * 128]),
                         reads=[t_pb], writes=[t_stg])
                k.dma(sp, dst[:, c0 * 128:(c0 + cn) * 128], stg[0:R, 0:cn * 128], reads=[t_stg])

        load_T(p1024_d, NP1024, 1024, lambda c, n: p1024[:, c:c + n, :], [t_p1024])
        load_T(p512_d, NP512, 512, lambda c, n: p512[:, c:c + n, :], [t_p512])
        load_T(p6144_d, NP6144, 6144, lambda c, n: p6144[:, c:c + n, :], [t_p6144])
        for l in range(DEPTH):
            c1 = lrud[:, :, l * 5 + 0]
            k.op(act, lambda e: e.activation(out=c1, in_=p512[:, :, 28 + l], func=AF.Exp, scale=-1.0), reads=[t_p512], writes=[t_lrud])
            k.op(act, lambda e: e.activation(out=c1, in_=c1, func=AF.Ln, bias=1.0), reads=[t_lrud], writes=[t_lrud])
            k.op(dve, lambda e: e.tensor_scalar(out=c1, in0=c1, scalar1=-8.0, scalar2=None, op0=ALU.mult), reads=[t_lrud], writes=[t_lrud])
            k.op(dve, lambda e: e.tensor_scalar(out=lrud[:, :, l * 5 + 1], in0=c1, scalar1=0.5, scalar2=None, op0=ALU.mult), reads=[t_lrud], writes=[t_lrud])
            k.op(dve, lambda e: e.tensor_scalar(out=lrud[:, :, l * 5 + 2], in0=p512[:, :, 20 + l], scalar1=0.5, scalar2=None, op0=ALU.mult), reads=[t_p512, t_lrud], writes=[t_lrud])
            k.op(dve, lambda e: e.tensor_scalar(out=lrud[:, :, l * 5 + 3], in0=p512[:, :, 24 + l], scalar1=0.5, scalar2=None, op0=ALU.mult), reads=[t_p512, t_lrud], writes=[t_lrud])
        k.op(act, lambda e: e.activation(out=hp[:, 32:64], in_=hp[:, 32:64], func=AF.Exp), reads=[t_hp], writes=[t_hp])
        k.op(dve, lambda e: e.tensor_scalar(out=hp[:, 32:64], in0=hp[:, 32:64], scalar1=-1.0, scalar2=None, op0=ALU.mult), reads=[t_hp], writes=[t_hp])

        x_res = sb("x_res", [128, 8, PT]); t_x = [Tk() for _ in range(8)]
        h_bf = sb("h_bf", [128, 8, PT], BF16); t_h = [Tk() for _ in range(8)]
        mix_bf = h_bf; t_mix = t_h
        sq_ring = sbring("sq", 2, [128, PT], BF16)
        rs_a = sb("rs_a", [128, PT]); t_rsa = Tk()
        rs_b = rs_a; t_rsb = t_rsa
        class LB:
            pass
        lbs = []
        for s_ in range(2):
            lb = LB()
            lb.xc = sb(f"L_xc{s_}", [128, PT]); lb.t_xc = Tk()
            lb.xcb = sb(f"L_xcb{s_}", [128, PT], BF16); lb.t_xcb = Tk()
            lb.tr = sb(f"L_tr{s_}", [128, PT]); lb.t_tr = Tk()
            lb.b2 = sb(f"L_b2{s_}", [128, PT]); lb.t_b2 = Tk()
            lb.ti = sb(f"L_ti{s_}", [128, PT]); lb.t_ti = Tk()
            lbs.append(lb)
        gg_ring = sbring("gg", 2, [128, PT])
        lru_y = sb("lru_y", [128, 4, PT], BF16); t_ly = [Tk() for _ in range(4)]
        U = sb("U", [128, 3 * 4 * PT])
        xs_f = U[:, 0:4 * PT].rearrange("p (c t) -> p c t", c=4); t_xs = [Tk() for _ in range(4)]
        zs = U[:, 4 * PT:8 * PT].rearrange("p (c t) -> p c t", c=4); t_zs = [Tk() for _ in range(4)]
        ysb = U[:, 8 * PT:12 * PT].rearrange("p (c t) -> p c t", c=4); t_ys = [Tk() for _ in range(4)]
        act_bf = U[:].bitcast(BF16)[:, 0:24 * PT].rearrange("p (c t) -> p c t", c=24); t_act = [Tk() for _ in range(24)]
        B_bf = sb("B_bf", [128, 2, PT], BF16); t_B = [Tk(), Tk()]
        C_bf = sb("C_bf", [128, 2, PT], BF16); t_C = [Tk(), Tk()]
        cv_xp = sbring("cv_xp", 2, [128, PT + 48])
        cv_acc = sbring("cv_acc", 2, [128, PT])
        SEGER = sb("SEGER", [128, 2048])
        S_seg = SEGER[:, 0:1024].rearrange("p (h t) -> p h t", h=8)
        S_ER = SEGER[:, 1024:2048].rearrange("p (h t) -> p h t", h=8)
        ggf = SEGER[:, :].rearrange("p (c t) -> p c t", c=4); t_ggf = [Tk() for _ in range(4)]
        dtT = sb("dtT", [128, 16, 8]); t_dtT = Tk()
        dAT = sb("dAT", [128, 16, 8]); t_dAT = Tk()
        dtmp = sb("dtmp", [128, 16, 8]); t_dtmp = Tk()
        onesf = sb("onesf", [128, 128]); t_onesf = Tk()
        k.op(pool, lambda e: e.memset(onesf[:], 1.0), writes=[t_onesf])
        S_dg = sb("S_dg", [128, 8, 128])
        csT = sb("csT", [128, 2, 8])
        S_MS = sb("S_MS", [128, 2, 128])
        S_ML = sb("S_ML", [128, 8, 128], BF16)
        S_Cs = sb("S_Cs", [128, 8, 128], BF16)
        S_wdt = sb("S_wdt", [128, 2, 8])
        SEGER2 = sb("SEGER2", [128, 2048])
        S_seg2 = SEGER2[:, 0:1024].rearrange("p (h t) -> p h t", h=8)
        S_ER2 = SEGER2[:, 1024:2048].rearrange("p (h t) -> p h t", h=8)
        S_MS2 = sb("S_MS2", [128, 2, 128])
        S_ML2 = sb("S_ML2", [128, 8, 128], BF16)
        S_Cs2 = sb("S_Cs2", [128, 8, 128], BF16)
        BSETS = [(S_seg, S_ER, S_MS, S_ML, S_Cs), (S_seg2, S_ER2, S_MS2, S_ML2, S_Cs2)]

        class Ctx:
            pass

        ctxs = []
        for s_ in range(2):
            cx = Ctx()
            cx.s = s_
            for nm in ("seg", "ER", "MS", "ML", "Cs", "dg", "csT", "wdtc", "xdt", "xw", "Btm", "hTbf", "hT", "sin", "sout"):
                setattr(cx, "t_" + nm, Tk())
            cx.xdt = sb(f"S_xdt{s_}", [128, 512], BF16)
            cx.xw = sb(f"S_xw{s_}", [128, 512], BF16)
            cx.Btm = sb(f"S_Btm{s_}", [128, 256], BF16)
            cx.hTbf = sb(f"hT_bf{s_}", [128, 512], BF16)
            cx.hTS = sb(f"hTS{s_}", [128, 512])
            cx.sin = sb(f"sst_in{s_}", [128, 4, 128])
            cx.sout = sb(f"sst_out{s_}", [128, 4, 128])
            ctxs.append(cx)
        hTP = sb("hTP", [128, DEPTH, 512]); t_hTP = [Tk() for _ in range(DEPTH)]
        lc_P = sb("lc_P", [128, DEPTH, 4, 1, 3]); lc_S = sb("lc_S", [128, 1, 4, DB, 3])
        lh_P = sb("lh_P", [128, DEPTH, 4, 1]); lh_S = sb("lh_S", [128, 1, 4, DB])
        sc_P = sb("sc_P", [128, DEPTH, 8, 1, 3]); sc_S = sb("sc_S", [128, 1, 8, DB, 3])
        fc_P = sb("fc_P", [128, DEPTH, 48, 1, 2]); fc_S = sb("fc_S", [128, 1, 48, DB, 2])
        t_lcP = [[Tk() for _ in range(4)] for _ in range(DEPTH)]; t_lcS = [[Tk() for _ in range(4)]]
        t_lhP = [[Tk() for _ in range(4)] for _ in range(DEPTH)]; t_lhS = [[Tk() for _ in range(4)]]
        t_scP = [[Tk() for _ in range(8)] for _ in range(DEPTH)]; t_scS = [[Tk() for _ in range(8)]]
        t_fcP = [[Tk() for _ in range(48)] for _ in range(DEPTH)]; t_fcS = [[Tk() for _ in range(48)]]
        for tns, tks in ((lc_P, t_lcP), (lh_P, t_lhP), (sc_P, t_scP), (fc_P, t_fcP)):
            k.op(pool, lambda e: e.memset(tns[:], 0.0), writes=[t for row in tks for t in row])
        k.op(pool, lambda e: e.memset(hTP[:], 0.0), writes=t_hTP)

        wslots = Ring([(sb(f"wslot{i}", [128, 4096], BF16), Tk()) for i in range(WSLOTS)])

        def wpieces(l):
            ps = []
            vin = w_in[l].rearrange("(k p) n -> p k n", p=128)
            for i in (2, 3, 4, 0, 1):
                ps.append(("in", 8, 512, vin[:, :, i * 512:(i + 1) * 512]))
            vout = w_out[l].rearrange("(k p) n -> p k n", p=128)
            for i in range(2):
                ps.append(("out", 8, 512, vout[:, :, i * 512:(i + 1) * 512]))
            vup = w_up[l].rearrange("(k p) n -> p k n", p=128)
            for i in range(6):
                ps.append(("up", 8, 512, vup[:, :, i * 512:(i + 1) * 512]))
                ps.append(("up", 8, 512, vup[:, :, DFF + i * 512:DFF + (i + 1) * 512]))
            vdn = w_down[l].rearrange("(k p) n -> p k n", p=128)
            for i in range(8):
                ps.append(("down", 24, 128, vdn[:, :, i * 128:(i + 1) * 128]))
            return ps

        NPIECE = 27
        wscr = nc.dram_tensor("wscr", [DEPTH * NPIECE, 128, 4096], BF16, kind="Internal").ap()
        t_scr = [Tk() for _ in range(DEPTH * NPIECE)]

        class WS:
            def __init__(self):
                self.issued = []
                self.pos = 0
                self.seq = []

            def set_plan(self, seq):
                self.seq = seq

            def _issue(self, i):
                name, KC, ncol, src, sidx, first = self.seq[i]
                slot, t_slot = wslots.next()
                n = KC * ncol
                view = slot[:, 0:n].rearrange("p (k n) -> p k n", k=KC)
                k.dma(sp, slot[:, 0:n], wscr[sidx][:, 0:n], reads=[t_scr[sidx]], writes=[t_slot])
                self.issued.append((view, t_slot, name))

            def get(self, name):
                while len(self.issued) < min(len(self.seq), self.pos + WSLOTS - 1):
                    self._issue(len(self.issued))
                v, t, n = self.issued[self.pos]
                assert n == name, (n, name)
                self.pos += 1
                return v, t

        ws = WS()

        def convert(l, lo, hi):
            for pi_, (nm, KC_, nc_, src_) in enumerate(wpieces(l)):
                if lo <= pi_ < hi:
                    sidx = l * NPIECE + pi_
                    k.dma(pool, wscr[sidx][:, 0:KC_ * nc_].rearrange("p (k n) -> p k n", k=KC_), src_, writes=[t_scr[sidx]])


        def rmsnorm(NT, nch, src_fn, src_tks, gain_fn, gain_tk, dst_fn, dst_tks):
            pb, t_pb = mmring.next()
            for c in range(nch):
                sq, t_sq = sq_ring.next()
                k.op(act, lambda e: e.activation(out=sq[:, :NT], in_=src_fn(c), func=AF.Square), reads=[src_tks[c]], writes=[t_sq])
                k.op(pe, lambda e: e.matmul(pb[:, :NT], lhsT=onesb[:], rhs=sq[:, :NT], start=(c == 0), stop=(c == nch - 1)),
                     reads=[t_sq, t_ones], writes=[t_pb])
            k.op(act, lambda e: e.activation(out=rs_a[:, :NT], in_=pb[:, :NT], func=AF.Ln, scale=1.0 / (nch * 128), bias=eps_col[:, 0:1]),
                 reads=[t_pb, t_eps], writes=[t_rsa])
            k.op(act, lambda e: e.activation(out=rs_b[:, :NT], in_=rs_a[:, :NT], func=AF.Exp, scale=-0.5), reads=[t_rsa], writes=[t_rsb])
            for c in range(nch):
                k.op(dve, lambda e: e.scalar_tensor_tensor(out=dst_fn(c), in0=src_fn(c), scalar=gain_fn(c), in1=rs_b[:, :NT], op0=ALU.mult, op1=ALU.mult),
                     reads=[src_tks[c], gain_tk, t_rsb], writes=[dst_tks[c]])

        eps_col = sb("eps_col", [128, 1]); t_eps = Tk()
        k.op(pool, lambda e: e.memset(eps_col[:], EPS), writes=[t_eps])

        def mm(NT, wv, t_w, col0, ncol, KC, rhs_fn, rhs_tks):
            pb, t_pb = mmring.next()
            for kk in range(KC):
                k.op(pe, lambda e: e.matmul(pb[0:ncol, :NT], lhsT=wv[:, kk, col0:col0 + ncol], rhs=rhs_fn(kk), start=(kk == 0), stop=(kk == KC - 1)),
                     reads=[t_w, rhs_tks[kk]], writes=[t_pb])
            return pb, t_pb

        def conv(cfg, pb, t_pb, Kc, st_view, t_st, wtab, t_wtab, wrow0, brow, ch, out_ap, t_out, xp, t_xpb):
            NT, nseq, L = cfg.NT, cfg.nseq, cfg.L
            W = L + Kc - 1
            xpv = xp[:, 0:nseq * W].rearrange("p (s w) -> p s w", s=nseq)
            pbv = pb[:, 0:NT].rearrange("p (s l) -> p s l", s=nseq)
            outv = out_ap.rearrange("p (s l) -> p s l", s=nseq)
            k.op(pool, lambda e: e.tensor_copy(out=xpv[:, :, 0:Kc - 1], in_=st_view), reads=[t_st], writes=[t_xpb])
            k.op(act, lambda e: e.copy(out=xpv[:, :, Kc - 1:W], in_=pbv), reads=[t_pb], writes=[t_xpb])
            k.op(act, lambda e: e.activation(out=outv, in_=pbv, func=AF.Identity, scale=wtab[:, ch, wrow0 + Kc - 1:wrow0 + Kc], bias=wtab[:, ch, brow:brow + 1]),
                 reads=[t_pb, t_wtab], writes=[t_out])
            for kk in range(Kc - 1):
                k.op(dve, lambda e: e.scalar_tensor_tensor(out=outv, in0=xpv[:, :, kk:kk + L], scalar=wtab[:, ch, wrow0 + kk:wrow0 + kk + 1], in1=outv, op0=ALU.mult, op1=ALU.add),
                     reads=[t_xpb, t_wtab, t_out], writes=[t_out])
            k.op(pool, lambda e: e.tensor_copy(out=st_view, in_=xpv[:, :, L:W]), reads=[t_xpb], writes=[t_st])

        def run_gens(gens):
            gens = list(gens)
            while gens:
                for g in list(gens):
                    try:
                        next(g)
                    except StopIteration:
                        gens.remove(g)

        NL_ = _DBG_LAYERS

        def layer(cfg, l, S, last_prompt):
            NT, nseq, L, Q = cfg.NT, cfg.nseq, cfg.L, cfg.Q
            isS = cfg.kind == "S"
            sl = 0 if isS else l
            lc, t_lc = (lc_S, t_lcS) if isS else (lc_P, t_lcP)
            lh, t_lh = (lh_S, t_lhS) if isS else (lh_P, t_lhP)
            sc, t_sc = (sc_S, t_scS) if isS else (sc_P, t_scP)
            fc, t_fc = (fc_S, t_fcS) if isS else (fc_P, t_fcP)
            if isS:
                load_T(st_lc[l], DB * 3, LW, lambda c, n: lc_S[:, 0, c:c + n].rearrange("p c b k -> p c (b k)"), t_lcS[0])
                load_T(st_lh[l], DB, LW, lambda c, n: lh_S[:, 0, c:c + n], t_lhS[0])
                load_T(st_sc[l], DB * 3, 1024, lambda c, n: sc_S[:, 0, c:c + n].rearrange("p c b k -> p c (b k)"), t_scS[0])
                load_T(st_fc[l], DB * 2, 2 * DFF, lambda c, n: fc_S[:, 0, c:c + n].rearrange("p c b k -> p c (b k)"), t_fcS[0])
            for hh in range(2):
                for ai, wsrc in enumerate((wa_d, wx_d)):
                    src = wsrc[l].rearrange("(j h) i o -> h i j o", h=2)[hh]
                    k.dma(pool, wbd[hh * 64:(hh + 1) * 64, :, ai, hh * 64:(hh + 1) * 64], src, writes=[t_wbd])

            if S:
                convert(l, 7, NPIECE)
            rmsnorm(NT, 8, lambda c: x_res[:, c, :NT], t_x, lambda c: p1024[:, c, l:l + 1], t_p1024, lambda c: h_bf[:, c, :NT], t_h)
            hfn = lambda kk: h_bf[:, kk, :NT]

            k.fence(act, t_act)
            k.fence(dve, t_act)
            w2, tw2 = ws.get("in")
            for j in range(4):
                pbz, t_pbz = mm(NT, w2, tw2, j * 128, 128, 8, hfn, t_h)
                k.op(act, lambda e: e.activation(out=zs[:, j, :NT], in_=pbz[:, :NT], func=AF.Silu), reads=[t_pbz], writes=[t_zs[j]])
            w3, tw3 = ws.get("in")
            w4, tw4 = ws.get("in")
            pend = []
            for c in range(8):
                wv, tw = (w3, tw3) if c < 4 else (w4, tw4)
                pbc, t_pbc = mm(NT, wv, tw, (c % 4) * 128, 128, 8, hfn, t_h)
                acc, t_acc = cv_acc.next()
                xpb, t_xpb = cv_xp.next()
                conv(cfg, pbc, t_pbc, 4, sc[:, sl, c], t_sc[sl][c], p1024, t_p1024, 9 + l * 4, 25 + l, c, acc[:, :NT], t_acc, xpb, t_xpb)
                if c < 4:
                    dst, t_dst = xs_f[:, c, :NT], t_xs[c]
                elif c < 6:
                    dst, t_dst = B_bf[:, c - 4, :NT], t_B[c - 4]
                else:
                    dst, t_dst = C_bf[:, c - 6, :NT], t_C[c - 6]
                while pend:
                    pend.pop(0)()
                pend.append(lambda dst=dst, acc=acc, t_acc=t_acc, t_dst=t_dst: k.op(act, lambda e: e.activation(out=dst, in_=acc[:, :NT], func=AF.Silu), reads=[t_acc], writes=[t_dst]))
            while pend:
                pend.pop(0)()
            ninst = cfg.ninst
            for i in range(ninst):
                for kk in range(8):
                    k.op(pe, lambda e: e.matmul(pb_m[0:Q, i * 8:(i + 1) * 8], lhsT=h_bf[:, kk, i * Q:(i + 1) * Q], rhs=wdt[:, l, kk, :], start=(kk == 0), stop=(kk == 7)),
                         reads=[t_h[kk], t_wdt], writes=[t_pbm])
            pmv = pb_m[0:Q, 0:ninst * 8].rearrange("p (i h) -> p i h", h=8)
            k.op(dve, lambda e: e.tensor_tensor(out=dtmp[0:Q, 0:ninst, :], in0=pmv, in1=hp[0:Q, l * 8:(l + 1) * 8].unsqueeze(1).to_broadcast([Q, ninst, 8]), op=ALU.add),
                 reads=[t_pbm, t_hp], writes=[t_dtmp])
            k.op(act, lambda e: e.activation(out=dtmp[0:Q, 0:ninst, :], in_=dtmp[0:Q, 0:ninst, :], func=AF.Exp), reads=[t_dtmp], writes=[t_dtmp])
            k.op(act, lambda e: e.activation(out=dtT[0:Q, 0:ninst, :], in_=dtmp[0:Q, 0:ninst, :], func=AF.Ln, bias=1.0), reads=[t_dtmp], writes=[t_dtT])
            k.op(dve, lambda e: e.tensor_tensor(out=dAT[0:Q, 0:ninst, :], in0=dtT[0:Q, 0:ninst, :], in1=hp[0:Q, 32 + l * 8:32 + (l + 1) * 8].unsqueeze(1).to_broadcast([Q, ninst, 8]), op=ALU.mult),
                 reads=[t_dtT, t_hp], writes=[t_dAT])
            for e_ in (act, dve, pool):
                k.fence(e_, t_ggf)

            w0, tw0 = ws.get("in")
            w1, tw1 = ws.get("in")

            def gen_lru(st):
                lb = lbs[st]
                L_xc, t_Lxc, L_xcb, t_Lxcb = lb.xc, lb.t_xc, lb.xcb, lb.t_xcb
                L_tr, t_Ltr, L_b2, t_Lb2, L_ti, t_Lti = lb.tr, lb.t_tr, lb.b2, lb.t_b2, lb.ti, lb.t_ti
                for j in (st, st + 2):
                    pbx, t_pbx_ = mm(NT, w0, tw0, j * 128, 128, 8, hfn, t_h)
                    pbg, t_pbg = mm(NT, w1, tw1, j * 128, 128, 8, hfn, t_h)
                    gg, t_gg = gg_ring.next()
                    k.op(act, lambda e: e.activation(out=gg[:, :NT], in_=pbg[:, :NT], func=AF.Gelu_apprx_tanh), reads=[t_pbg], writes=[t_gg])
                    L_xp, t_Lxp = cv_xp.next()
                    conv(cfg, pbx, t_pbx_, 4, lc[:, sl, j], t_lc[sl][j], p512, t_p512, l * 4, 16 + l, j, L_xc[:, :NT], t_Lxc, L_xp, t_Lxp)
                    if isS:
                        k.op(pool, lambda e: e.tensor_copy(out=L_xcb[:, :NT], in_=L_xc[:, :NT]), reads=[t_Lxc], writes=[t_Lxcb])
                    else:
                        k.op(act, lambda e: e.copy(out=L_xcb[:, :NT], in_=L_xc[:, :NT]), reads=[t_Lxc], writes=[t_Lxcb])
                    yield
                    pbr, t_pbr_ = mmring.next()
                    k.op(pe, lambda e: e.matmul(pbr[:, :NT], lhsT=wbd[:, j, 0, :], rhs=L_xcb[:, :NT], start=True, stop=True), reads=[t_wbd, t_Lxcb], writes=[t_pbr_])
                    pbi, t_pbi = mmring.next()
                    k.op(pe, lambda e: e.matmul(pbi[:, :NT], lhsT=wbd[:, j, 1, :], rhs=L_xcb[:, :NT], start=True, stop=True), reads=[t_wbd, t_Lxcb], writes=[t_pbi])
                    c1 = lrud[:, j, l * 5 + 0:l * 5 + 1]
                    c1h = lrud[:, j, l * 5 + 1:l * 5 + 2]
                    hba = lrud[:, j, l * 5 + 2:l * 5 + 3]
                    hbx = lrud[:, j, l * 5 + 3:l * 5 + 4]
                    k.op(act, lambda e: e.activation(out=L_tr[:, :NT], in_=pbr[:, :NT], func=AF.Tanh, scale=0.5, bias=hba), reads=[t_pbr_, t_lrud], writes=[t_Ltr])
                    k.op(act, lambda e: e.activation(out=L_ti[:, :NT], in_=pbi[:, :NT], func=AF.Tanh, scale=0.5, bias=hbx), reads=[t_pbi, t_lrud], writes=[t_Lti])
                    yield
                    k.op(act, lambda e: e.activation(out=L_b2[:, :NT], in_=L_tr[:, :NT], func=AF.Exp, scale=c1, bias=c1), reads=[t_Ltr, t_lrud], writes=[t_Lb2])
                    k.op(act, lambda e: e.activation(out=L_tr[:, :NT], in_=L_tr[:, :NT], func=AF.Exp, scale=c1h, bias=c1h), reads=[t_Ltr, t_lrud], writes=[t_Ltr])
                    k.op(dve, lambda e: e.scalar_tensor_tensor(out=L_ti[:, :NT], in0=L_ti[:, :NT], scalar=1.0, in1=L_xc[:, :NT], op0=ALU.add, op1=ALU.mult), reads=[t_Lti, t_Lxc], writes=[t_Lti])
                    yield
                    k.op(act, lambda e: e.activation(out=L_b2[:, :NT], in_=L_b2[:, :NT], func=AF.Ln, scale=-0.25, bias=q_col[:, 0:1]), reads=[t_Lb2, t_eps], writes=[t_Lb2])
                    k.op(act, lambda e: e.activation(out=L_b2[:, :NT], in_=L_b2[:, :NT], func=AF.Exp, scale=0.5), reads=[t_Lb2], writes=[t_Lb2])
                    yield
                    k.op(dve, lambda e: e.tensor_tensor(out=L_ti[:, :NT], in0=L_ti[:, :NT], in1=L_b2[:, :NT], op=ALU.mult), reads=[t_Lti, t_Lb2], writes=[t_Lti])
                    for s_ in range(nseq):
                        k.op(dve, lambda e: e.tensor_tensor_scan(out=L_xc[:, s_ * L:(s_ + 1) * L], data0=L_tr[:, s_ * L:(s_ + 1) * L], data1=L_ti[:, s_ * L:(s_ + 1) * L],
                                                                 initial=lh[:, sl, j, s_:s_ + 1], op0=ALU.mult, op1=ALU.add),
                             reads=[t_Ltr, t_Lti, t_lh[sl][j]], writes=[t_Lxc])
                    yield
                    k.op(pool, lambda e: e.tensor_copy(out=lh[:, sl, j, :], in_=L_xc[:, :NT].rearrange("p (s l) -> p s l", s=nseq)[:, :, L - 1]), reads=[t_Lxc], writes=[t_lh[sl][j]])
                    k.op(dve, lambda e: e.tensor_tensor(out=lru_y[:, j, :NT], in0=L_xc[:, :NT], in1=gg[:, :NT], op=ALU.mult), reads=[t_Lxc, t_gg], writes=[t_ly[j]])
                    yield

            two = isS
            def inst_parts(i, cx):
                st = cx.s
                o3 = st * 64 if two else 0
                cs = slice(i * Q, (i + 1) * Q)
                bx_i = 6 if (two and st == 1) else 4
                pbxx, t_pbxx = banks[bx_i], t_bank[bx_i]
                if Q == 128:
                    Rh = [(banks[5], t_bank[5], 0), (banks[6], t_bank[6], 0)]
                else:
                    Rh = [(banks[5], t_bank[5], o3)]
                if two and st == 1:
                    cs_ps = pb_m[0:Q, 168:176]
                    sc_ps2 = pb_m[0:Q, 176:176 + 2 * Q]
                    Btm_ps, t_Btmps = banks[5][0:Q, 256:512], t_bank[5]
                else:
                    cs_ps = pb_m[0:Q, 128:136]
                    sc_ps2 = pb_m[0:Q, 0:256] if Q == 128 else pb_m[0:Q, 136:136 + 2 * Q]
                    Btm_ps, t_Btmps = pb_m[0:Q, 256:512], t_pbm
                bset = BSETS[0] if two else BSETS[i % 2]
                cb = cx if two else ctxs[i % 2]
                seg = bset[0][:, :, o3:o3 + Q]
                ER = bset[1][:, :, o3:o3 + Q]
                MS = bset[2][:, :, o3:o3 + Q]
                ML = bset[3][:, :, o3:o3 + Q]
                Cs = bset[4][:, :, o3:o3 + Q]
                dg = S_dg[:, :, o3:o3 + Q]
                csS = csT[:, cb.s, :]
                wdc = S_wdt[:, cb.s, :]
                Btm = cb.Btm[:, :]
                t_Btm = cb.t_Btm
                if isS:
                    hTv, t_hT = cx.hTS[:], cx.t_hT
                else:
                    hTv, t_hT = hTP[:, l, :], t_hTP[l]

                def S0():
                    if isS:
                        k.dma(sp, cx.sin[:], st_ss[l, i].rearrange("(c r) n -> r c n", r=128), writes=[cx.t_sin])
                        pbs, t_pbs = mmring.next()
                        for c in range(4):
                            k.op(pe, lambda e: e.transpose(pbs[:, c * 128:(c + 1) * 128], cx.sin[:, c, :], ident[:]), reads=[cx.t_sin, t_ident], writes=[t_pbs])
                        k.op(dve, lambda e: e.tensor_copy(out=hTv, in_=pbs[:]), reads=[t_pbs], writes=[t_hT])

                def A1():
                    for j in range(4):
                        k.op(pe, lambda e: e.transpose(pbxx[0:Q, j * 128:(j + 1) * 128], xs_f[:, j, cs], ident[:]), reads=[t_xs[j], t_ident], writes=[t_pbxx])
                    for g in range(2):
                        k.op(pe, lambda e: e.matmul(Btm_ps[:, g * 128:(g + 1) * 128], lhsT=B_bf[:, g, cs], rhs=identb[:], start=True, stop=True),
                             reads=[t_B[g], t_identb], writes=[t_Btmps])
                    k.op(pe, lambda e: e.matmul(cs_ps, lhsT=tri[0:Q, 0:Q], rhs=dAT[0:Q, i, :], start=True, stop=True), reads=[t_tri, t_dAT], writes=[t_pbm])
                    k.op(act, lambda e: e.copy(out=csS[0:Q, :], in_=cs_ps), reads=[t_pbm], writes=[cb.t_csT])
                    k.op(act, lambda e: e.copy(out=Btm[0:Q, :], in_=Btm_ps), reads=[t_Btmps], writes=[t_Btm])
                    k.op(dve, lambda e: e.tensor_tensor(out=dg[0:Q], in0=csS[0:Q, :].unsqueeze(2).to_broadcast([Q, 8, Q]), in1=ident[0:Q, 0:Q].unsqueeze(1).to_broadcast([Q, 8, Q]), op=ALU.mult),
                         reads=[cb.t_csT, t_ident], writes=[cx.t_dg])

                def A2():
                    if Q == 128:
                        for half in range(2):
                            rb, t_rb, _ = Rh[half]
                            k.op(pe, lambda e: e.matmul(rb[:, 0:512], lhsT=onesf[0:Q, :], rhs=dg[0:Q, half * 4:(half + 1) * 4, :], start=True, stop=True),
                                 reads=[t_onesf, cx.t_dg], writes=[t_rb])
                    else:
                        rb, t_rb, ro = Rh[0]
                        k.op(pe, lambda e: e.matmul(rb[:, ro:ro + 8 * Q], lhsT=onesf[0:Q, :], rhs=dg[0:Q], start=True, stop=True), reads=[t_onesf, cx.t_dg], writes=[t_rb])
                    for g in range(2):
                        k.op(pe, lambda e: e.matmul(sc_ps2[:, g * Q:(g + 1) * Q], lhsT=B_bf[:, g, cs], rhs=C_bf[:, g, cs], start=True, stop=True),
                             reads=[t_B[g], t_C[g]], writes=[t_pbm])

                def B1():
                    k.op(dve, lambda e: e.tensor_tensor(out=MS[0:Q], in0=sc_ps2.rearrange("p (g t) -> p g t", g=2), in1=tri[0:Q, 0:Q].unsqueeze(1).to_broadcast([Q, 2, Q]), op=ALU.mult),
                         reads=[t_pbm, t_tri], writes=[cb.t_MS])
                    if Q == 128:
                        for half in range(2):
                            rb, t_rb, _ = Rh[half]
                            hs = slice(half * 4, (half + 1) * 4)
                            k.op(dve, lambda e: e.tensor_tensor(out=seg[0:Q, hs], in0=rb[0:Q, 0:512].rearrange("p (h t) -> p h t", h=4),
                                                                in1=csS[0:Q, hs].unsqueeze(2).to_broadcast([Q, 4, Q]), op=ALU.subtract),
                                 reads=[t_rb, cb.t_csT], writes=[cb.t_seg])
                            k.op(act, lambda e: e.activation(out=ER[:, hs], in_=rb[:, 0:512].rearrange("p (h t) -> p h t", h=4), func=AF.Exp),
                                 reads=[t_rb], writes=[cb.t_ER])
                    else:
                        rb, t_rb, ro = Rh[0]
                        rv = rb[:, ro:ro + 8 * Q].rearrange("p (h t) -> p h t", h=8)
                        k.op(dve, lambda e: e.tensor_tensor(out=seg[0:Q], in0=rv[0:Q], in1=csS[0:Q, :].unsqueeze(2).to_broadcast([Q, 8, Q]), op=ALU.subtract),
                             reads=[t_rb, cb.t_csT], writes=[cb.t_seg])
                        k.op(act, lambda e: e.activation(out=ER, in_=rv, func=AF.Exp), reads=[t_rb], writes=[cb.t_ER])
                    k.op(act, lambda e: e.activation(out=wdc[0:Q, :], in_=seg[0:Q, :, Q - 1], func=AF.Exp), reads=[cb.t_seg], writes=[cb.t_wdtc])
                    k.op(dve, lambda e: e.tensor_tensor(out=wdc[0:Q, :], in0=wdc[0:Q, :], in1=dtT[0:Q, i, :], op=ALU.mult), reads=[t_dtT, cb.t_wdtc], writes=[cb.t_wdtc])
                    pxv = pbxx[0:Q, :].rearrange("p (h d) -> p h d", h=8)
                    k.op(dve, lambda e: e.tensor_tensor(out=cb.xdt[0:Q, :].rearrange("p (h d) -> p h d", h=8), in0=pxv, in1=dtT[0:Q, i, :].unsqueeze(2).to_broadcast([Q, 8, 64]), op=ALU.mult),
                         reads=[t_pbxx, t_dtT], writes=[cb.t_xdt])
                    k.op(dve, lambda e: e.tensor_tensor(out=cb.xw[0:Q, :].rearrange("p (h d) -> p h d", h=8), in0=pxv, in1=wdc[0:Q, :].unsqueeze(2).to_broadcast([Q, 8, 64]), op=ALU.mult),
                         reads=[t_pbxx, cb.t_wdtc], writes=[cb.t_xw])

                def B2():
                    k.op(dve, lambda e: e.tensor_scalar(out=seg[0:Q], in0=seg[0:Q], scalar1=0.0, scalar2=None, op0=ALU.min), reads=[cb.t_seg], writes=[cb.t_seg])
                    k.op(act, lambda e: e.activation(out=seg[0:Q], in_=seg[0:Q], func=AF.Exp), reads=[cb.t_seg], writes=[cb.t_seg])
                    for g in range(2):
                        k.op(pool, lambda e: e.tensor_tensor(out=Cs[:, g * 4:(g + 1) * 4], in0=ER[:, g * 4:(g + 1) * 4],
                                                             in1=C_bf[:, g, cs].unsqueeze(1).to_broadcast([128, 4, Q]), op=ALU.mult),
                             reads=[cb.t_ER, t_C[g]], writes=[cb.t_Cs])
                    for g in range(2):
                        k.op(dve, lambda e: e.tensor_tensor(out=ML[0:Q, g * 4:(g + 1) * 4], in0=seg[0:Q, g * 4:(g + 1) * 4],
                                                            in1=MS[0:Q, g].unsqueeze(1).to_broadcast([Q, 4, Q]), op=ALU.mult),
                             reads=[cb.t_seg, cb.t_MS], writes=[cb.t_ML])

                def C():
                    if isS:
                        k.op(pool, lambda e: e.tensor_copy(out=cb.hTbf[:], in_=hTv), reads=[t_hT], writes=[cb.t_hTbf])
                    else:
                        k.op(act, lambda e: e.copy(out=cb.hTbf[:], in_=hTv), reads=[t_hT], writes=[cb.t_hTbf])
                    pby, t_pby = mmring.next()
                    for h in range(8):
                        j, hh = h // 2, h % 2
                        o = pby[hh * 64:(hh + 1) * 64, j * Q:(j + 1) * Q]
                        k.op(pe, lambda e: e.matmul(o, lhsT=cb.xdt[0:Q, h * 64:(h + 1) * 64], rhs=ML[0:Q, h], start=True, stop=False), reads=[cb.t_xdt, cb.t_ML], writes=[t_pby])
                        k.op(pe, lambda e: e.matmul(o, lhsT=cb.hTbf[:, h * 64:(h + 1) * 64], rhs=Cs[:, h], start=False, stop=True), reads=[cb.t_hTbf, cb.t_Cs], writes=[t_pby])
                    pbu, t_pbu = mmring.next()
                    for g in range(2):
                        k.op(pe, lambda e: e.matmul(pbu[:, g * 256:(g + 1) * 256], lhsT=Btm[0:Q, g * 128:(g + 1) * 128], rhs=cb.xw[0:Q, g * 256:(g + 1) * 256], start=True, stop=True),
                             reads=[t_Btm, cb.t_xw], writes=[t_pbu])
                    for j in range(4):
                        k.op(dve, lambda e: e.scalar_tensor_tensor(out=ysb[:, j, cs], in0=xs_f[:, j, cs], scalar=p512[:, j, 40 + l:41 + l], in1=pby[:, j * Q:(j + 1) * Q], op0=ALU.mult, op1=ALU.add),
                             reads=[t_xs[j], t_p512, t_pby], writes=[t_ys[j]])
                    k.op(dve, lambda e: e.tensor_tensor(out=hTv.rearrange("p (h d) -> p h d", h=8), in0=hTv.rearrange("p (h d) -> p h d", h=8),
                                                        in1=ER[:, :, Q - 1].unsqueeze(2).to_broadcast([128, 8, 64]), op=ALU.mult),
                         reads=[cb.t_ER, t_hT, cb.t_hTbf], writes=[t_hT])
                    k.op(dve, lambda e: e.tensor_tensor(out=hTv, in0=hTv, in1=pbu[:, :], op=ALU.add), reads=[t_pbu, t_hT], writes=[t_hT])
                    if isS or (last_prompt and i == ninst - 1):
                        dst = o_sss[l, i] if isS else o_pss[l]
                        pbs, t_pbs = mmring.next()
                        for c in range(4):
                            k.op(pe, lambda e: e.transpose(pbs[:, c * 128:(c + 1) * 128], hTv[:, c * 128:(c + 1) * 128], ident[:]), reads=[t_hT, t_ident], writes=[t_pbs])
                        k.op(act, lambda e: e.copy(out=cx.sout[:].rearrange("p c n -> p (c n)"), in_=pbs[:]), reads=[t_pbs], writes=[cx.t_sout])
                        k.dma(sp, dst.rearrange("(c r) n -> r c n", r=128), cx.sout[:], reads=[cx.t_sout])

                return S0, A1, A2, B1, B2, C

            def gen_inst(i, cx):
                S0, A1, A2, B1, B2, C = inst_parts(i, cx)
                S0()
                yield
                A1()
                yield
                A2()
                yield
                B1()
                yield
                B2()
                yield
                C()
                yield

            pst = {"fnext": 0, "fdone": 0, "bnext": 0, "bdone": 0}

            def gen_front(cx):
                parts = [inst_parts(i, cx) for i in range(ninst)]
                pst["parts"] = parts
                for i in range(ninst):
                    pst["fnext"] = i
                    yield
                    parts[i][0](); parts[i][1]()
                    yield
                    parts[i][2]()
                    yield
                    parts[i][3]()
                    pst["fdone"] = i + 1
                pst["fnext"] = ninst
                yield

            def gen_back(cx):
                for i in range(ninst):
                    pst["bnext"] = i
                    yield
                    parts = pst["parts"]
                    parts[i][4]()
                    yield
                    parts[i][5]()
                    pst["bdone"] = i + 1
                pst["bnext"] = ninst
                yield

            def run_sched(items):
                items = [list(it) for it in items]
                while items:
                    cand = [it for it in items if it[1]()]
                    assert cand, "scheduler deadlock"
                    it = min(cand, key=lambda z: z[2])
                    try:
                        next(it[0])
                        it[2] = k.last_fin
                    except StopIteration:
                        items.remove(it)

            def gen_stream(st, step):
                for i in range(st, ninst, step):
                    yield from gen_inst(i, ctxs[st])

            yes = lambda: True
            t0_ = k.last_fin
            if two:
                run_gens([gen_lru(0), gen_lru(1), gen_stream(0, 2), gen_stream(1, 2)])
            else:
                f_ready = lambda: pst["fnext"] < 2 or pst["bdone"] >= pst["fnext"] - 1
                b_ready = lambda: pst["bnext"] < pst["fdone"] or pst["bnext"] >= ninst
                run_sched([[gen_lru(0), yes, t0_], [gen_lru(1), yes, t0_], [gen_front(ctxs[0]), f_ready, t0_], [gen_back(ctxs[0]), b_ready, t0_]])

            rmsnorm(NT, 4, lambda c: lru_y[:, c, :NT], t_ly, lambda c: p512[:, c, 32 + l:33 + l], t_p512, lambda c: mix_bf[:, c, :NT], t_mix[0:4])
            for j in range(4):
                k.op(pool, lambda e: e.tensor_tensor(out=ysb[:, j, :NT], in0=ysb[:, j, :NT], in1=zs[:, j, :NT], op=ALU.mult), reads=[t_ys[j], t_zs[j]], writes=[t_ys[j]])
            rmsnorm(NT, 4, lambda c: ysb[:, c, :NT], t_ys, lambda c: p512[:, c, 36 + l:37 + l], t_p512, lambda c: mix_bf[:, 4 + c, :NT], t_mix[4:8])

            if isS or last_prompt:
                R3 = nseq * 3
                store_T(lambda c: lc[:, sl, c].rearrange("p b k -> p (b k)"), t_lc[sl], R3, 4, (o_slc if isS else o_plc)[l])
                store_T(lambda c: lh[:, sl, c], t_lh[sl], nseq, 4, (o_slh if isS else o_plh)[l])
                store_T(lambda c: sc[:, sl, c].rearrange("p b k -> p (b k)"), t_sc[sl], R3, 8, (o_ssc if isS else o_psc)[l])

            mfn = lambda kk: mix_bf[:, kk, :NT]
            for pi in range(2):
                wv, tw = ws.get("out")
                for m in range(4):
                    pbo, t_pbo = mm(NT, wv, tw, m * 128, 128, 8, mfn, t_mix)
                    mm_ = pi * 4 + m
                    k.op(dve, lambda e: e.tensor_tensor(out=x_res[:, mm_, :NT], in0=x_res[:, mm_, :NT], in1=pbo[:, :NT], op=ALU.add), reads=[t_pbo, t_x[mm_]], writes=[t_x[mm_]])

            if S and l + 1 < NL_:
                convert(l + 1, 0, 7)
            rmsnorm(NT, 8, lambda c: x_res[:, c, :NT], t_x, lambda c: p1024[:, c, 4 + l:5 + l], t_p1024, lambda c: h_bf[:, c, :NT], t_h)
            k.fence(dve, t_xs + t_zs + t_ys)
            k.fence(act, [ctxs[0].t_seg, ctxs[0].t_ER, ctxs[1].t_seg, ctxs[1].t_ER])
            for pi in range(6):
                wg, twg = ws.get("up")
                wvv, twv = ws.get("up")
                for m in range(4):
                    ch = pi * 4 + m
                    pbg, t_pbg = mm(NT, wg, twg, m * 128, 128, 8, hfn, t_h)
                    acc, t_acc = cv_acc.next()
                    xpb, t_xpb = cv_xp.next()
                    conv(cfg, pbg, t_pbg, 3, fc[:, sl, ch], t_fc[sl][ch], p6144, t_p6144, l * 3, 12 + l, ch, acc[:, :NT], t_acc, xpb, t_xpb)
                    while pend:
                        pend.pop(0)()
                    pend.append(lambda m=m, acc=acc, t_acc=t_acc: k.op(act, lambda e: e.activation(out=ggf[:, m, :NT], in_=acc[:, :NT], func=AF.Gelu_apprx_tanh), reads=[t_acc], writes=[t_ggf[m]]))
                for m in range(4):
                    ch = pi * 4 + m
                    pbv, t_pbv = mm(NT, wvv, twv, m * 128, 128, 8, hfn, t_h)
                    acc, t_acc = cv_acc.next()
                    xpb, t_xpb = cv_xp.next()
                    conv(cfg, pbv, t_pbv, 3, fc[:, sl, 24 + ch], t_fc[sl][24 + ch], p6144, t_p6144, l * 3, 12 + l, 24 + ch, acc[:, :NT], t_acc, xpb, t_xpb)
                    while pend:
                        pend.pop(0)()
                    k.op(dve, lambda e: e.tensor_tensor(out=act_bf[:, ch, :NT], in0=acc[:, :NT], in1=ggf[:, m, :NT], op=ALU.mult), reads=[t_acc, t_ggf[m]], writes=[t_act[ch]])
            if isS or last_prompt:
                store_T(lambda c: fc[:, sl, c].rearrange("p b k -> p (b k)"), t_fc[sl], nseq * 2, 48, (o_sfc if isS else o_pfc)[l])
            afn = lambda kk: act_bf[:, kk, :NT]
            for m in range(8):
                wv, tw = ws.get("down")
                pbo, t_pbo = mm(NT, wv, tw, 0, 128, 24, afn, t_act)
                k.op(dve, lambda e: e.tensor_tensor(out=x_res[:, m, :NT], in0=x_res[:, m, :NT], in1=pbo[:, :NT], op=ALU.add), reads=[t_pbo, t_x[m]], writes=[t_x[m]])

        q_col = sb("q_col", [128, 1])
        k.op(pool, lambda e: e.memset(q_col[:], 0.25), writes=[t_eps])

        tiles = [TileCfg("S", DB * DS, DB, DS, DS), TileCfg("M", NMETA, 1, NMETA, NMETA)]
        for i in range(SEQ // PT):
            tiles.append(TileCfg("P", PT, 1, PT, 128, col0=i * PT))
        if _DBG_TILES is not None:
            tiles = [tiles[i] for i in _DBG_TILES]
        NL = _DBG_LAYERS
        seq = []
        for ti_ in range(len(tiles)):
            for l in range(NL):
                for pi_, (nm, KC_, nc_, src_) in enumerate(wpieces(l)):
                    seq.append((nm, KC_, nc_, src_, l * NPIECE + pi_, ti_ == 0))
        ws.set_plan(seq)

        convert(0, 0, 7)
        for ti, cfg in enumerate(tiles):
            NT = cfg.NT
            if cfg.kind == "M":
                load_T(meta_d, NMETA, D, lambda c, n: x_res[:, c:c + n, 0:NMETA], t_x)
            elif cfg.kind == "P":
                for b in range(PT // 128):
                    r0 = cfg.col0 + b * 128
                    load_T(xp_d[r0:r0 + 128, :], 128, D, lambda c, n: x_res[:, c:c + n, b * 128:(b + 1) * 128], t_x)
            else:
                load_T(xs_d, DB * DS, D, lambda c, n: x_res[:, c:c + n, 0:DB * DS], t_x)
            last_prompt = cfg.kind == "P" and cfg.col0 + PT == SEQ
            for l in range(NL):
                layer(cfg, l, ti == 0, last_prompt)
            if cfg.kind == "M":
                continue
            rmsnorm(NT, 8, lambda c: x_res[:, c, :NT], t_x, lambda c: p1024[:, c, 8:9], t_p1024, lambda c: x_res[:, c, :NT], t_x)
            if cfg.kind == "P":
                for b in range(PT // 128):
                    r0 = cfg.col0 + b * 128
                    store_T(lambda c: x_res[:, c, b * 128:(b + 1) * 128], t_x, 128, 8, yp_o[r0:r0 + 128, :])
            else:
                store_T(lambda c: x_res[:, c, 0:NT], t_x, NT, 8, ys_o)
        k.finish(sp)
        if _DBG_PRINT:
            print('SBUF bytes remaining/partition:', nc.sbuf_bytes_remaining // 128 if nc.sbuf_bytes_remaining > 300000 else nc.sbuf_bytes_remaining)
    return nc


_PROG = None


def _prep_inputs(inp):
    f = lambda a: np.ascontiguousarray(np.asarray(a, dtype=np.float32))
    p1024 = np.concatenate([f(inp["norm_mix"]), f(inp["norm_ffn"]), f(inp["norm_final"])[None], f(inp["ssd_conv_w"]).reshape(16, 1024), f(inp["ssd_conv_b"])], 0)
    p512 = np.concatenate([f(inp["lru_conv_w"]).reshape(16, 512), f(inp["lru_conv_b"]), f(inp["lru_ba"]), f(inp["lru_bx"]), f(inp["lru_a_param"]),
                           f(inp["lru_out_norm"]), f(inp["ssd_out_norm"]), np.repeat(f(inp["ssd_d"]), HP, axis=1)], 0)
    p6144 = np.concatenate([f(inp["ffn_conv_w"]).reshape(12, 6144), f(inp["ffn_conv_b"])], 0)
    hp = np.tile(np.concatenate([f(inp["ssd_dt_bias"]).reshape(-1), f(inp["ssd_a_log"]).reshape(-1)])[None, :], (128, 1))
    common = {
        "meta": f(inp["meta_tokens"]), "w_in": f(inp["w_in"]), "w_out": f(inp["w_out"]), "w_up": f(inp["ffn_w_up"]), "w_down": f(inp["ffn_w_down"]),
        "wa": f(inp["lru_wa"]), "wx": f(inp["lru_wx"]), "p1024": f(p1024), "p512": f(p512), "p6144": f(p6144), "hp": f(hp),
        "ident": np.eye(128, dtype=np.float32), "tri": np.triu(np.ones((128, 128), np.float32)),
    }
    maps = []
    for c in range(NCORES):
        b0, b1 = c * DB, (c + 1) * DB
        m = dict(common)
        m["xp"] = f(inp["x_prompt"][c])
        m["xs"] = f(inp["x_sample"][b0:b1]).reshape(DB * DS, D)
        m["st_lc"] = f(inp["state_lru_conv"][:, b0:b1]).reshape(DEPTH, DB * 3, LW)
        m["st_lh"] = f(inp["state_lru_h"][:, b0:b1])
        m["st_sc"] = f(inp["state_ssd_conv"][:, b0:b1]).reshape(DEPTH, DB * 3, 1024)
        m["st_ss"] = f(inp["state_ssd"][:, b0:b1]).reshape(DEPTH, DB, SI, SN)
        m["st_fc"] = f(inp["state_ffn_conv"][:, b0:b1]).reshape(DEPTH, DB * 2, 2 * DFF)
        maps.append(m)
    return maps


def kernel(**inputs):
    global _PROG
    if _PROG is None:
        _PROG = build_program()
    maps = _prep_inputs(inputs)
    res = run_bass_kernel_spmd(_PROG, maps, core_ids=list(range(NCORES)))
    R = res.results
    cat = lambda key, ax: np.concatenate([np.asarray(r[key]) for r in R], axis=ax)
    y_prompt = np.stack([np.asarray(r["yp"]) for r in R], 0)
    y_sample = cat("ys", 0).reshape(NCORES * DB, DS, D)
    p_lc = np.stack([np.asarray(r["o_plc"]) for r in R], 1)
    p_lh = np.stack([np.asarray(r["o_plh"])[:, 0] for r in R], 1)
    p_sc = np.stack([np.asarray(r["o_psc"]) for r in R], 1)
    p_ss = np.stack([np.asarray(r["o_pss"]).reshape(DEPTH, NH, HP, SN) for r in R], 1)
    p_fc = np.stack([np.asarray(r["o_pfc"]) for r in R], 1)
    s_lc = cat("o_slc", 1).reshape(DEPTH, NCORES * DB, 3, LW)
    s_lh = cat("o_slh", 1)
    s_sc = cat("o_ssc", 1).reshape(DEPTH, NCORES * DB, 3, 1024)
    s_ss = cat("o_sss", 1).reshape(DEPTH, NCORES * DB, NH, HP, SN)
    s_fc = cat("o_sfc", 1).reshape(DEPTH, NCORES * DB, 2, 2 * DFF)
    outs = (y_prompt, y_sample, p_lc, p_lh, p_sc, p_ss, p_fc, s_lc, s_lh, s_sc, s_ss, s_fc)
    return tuple(np.ascontiguousarray(o, dtype=np.float32) for o in outs)
```
